# Optimizing a Trainium2 kernel written in Bass

```python
import math
import jax, jax.numpy as jnp
from jax import lax
import numpy as np


D_MODEL = 2048
BATCH = 1
SEQ = 16384
DEPTH = 2

GRID_W = 64
CTX_LEN = 256
HEAD_DIM = 128
DA_HEADS = 4
NA_HEADS = 8
DA_WIDTH = DA_HEADS * 2 * HEAD_DIM
NA_WIDTH = NA_HEADS * HEAD_DIM
MIX_DIM = DA_WIDTH + NA_WIDTH
IN_DIM = 3 * DA_WIDTH + 3 * NA_WIDTH
IN_SPLITS = [DA_WIDTH, 2 * DA_WIDTH, 3 * DA_WIDTH, 3 * DA_WIDTH + NA_WIDTH, 3 * DA_WIDTH + 2 * NA_WIDTH]
NA_KH = 8
NA_KW = 16
D_FF = 5632
CONV_W = 3
ROPE_THETA = 10000.0
Q_BLOCK = 128
LN_EPS = 1e-5
ATTN_SCALE = HEAD_DIM ** -0.5
N_MOD = 6

kernel_name = 'hybrid_diffattn_natten_convffn_deepnorm'


def _layernorm(x, g, b):
    xf = x.astype(jnp.float32)
    mu = jnp.mean(xf, axis=-1, keepdims=True)
    var = jnp.mean(jnp.square(xf - mu), axis=-1, keepdims=True)
    y = (xf - mu) * lax.rsqrt(var + LN_EPS)
    return (y * g.astype(jnp.float32) + b.astype(jnp.float32)).astype(x.dtype)


def _rmsnorm(x, g):
    xf = x.astype(jnp.float32)
    y = xf * lax.rsqrt(jnp.mean(jnp.square(xf), axis=-1, keepdims=True) + LN_EPS)
    return (y * g.astype(jnp.float32)).astype(x.dtype)


def _to_blocks(t, qb):
    b, l = t.shape[:2]
    return jnp.moveaxis(t.reshape((b, l // qb, qb) + t.shape[2:]), 1, 0)


def _from_blocks(t):
    nb, b, qb = t.shape[:3]
    return jnp.moveaxis(t, 0, 1).reshape((b, nb * qb) + t.shape[3:])


def _axial_rope_tables(L):
    t = jnp.arange(L, dtype=jnp.int32)
    pos_r = (t // GRID_W).astype(jnp.float32)
    pos_c = (t % GRID_W).astype(jnp.float32)
    half = HEAD_DIM // 2
    inv_freq = 1.0 / (ROPE_THETA ** (jnp.arange(0, half, 2, dtype=jnp.float32) / half))
    ar = pos_r[:, None] * inv_freq[None, :]
    ac = pos_c[:, None] * inv_freq[None, :]
    ang = jnp.concatenate([ar, ar, ac, ac], axis=-1)
    return jnp.cos(ang), jnp.sin(ang)


def _apply_rope(x, cos, sin):
    cos = cos[:, None, None, :].astype(x.dtype)
    sin = sin[:, None, None, :].astype(x.dtype)
    a1, a2, b1, b2 = jnp.split(x, 4, axis=-1)
    rot = jnp.concatenate([-a2, a1, -b2, b1], axis=-1)
    return x * cos + rot * sin


def _na_neighbours(L):
    rows = L // GRID_W
    kh = min(NA_KH, rows)
    t = jnp.arange(L, dtype=jnp.int32)
    r = t // GRID_W
    col = t % GRID_W
    rs = jnp.clip(r - kh // 2, 0, rows - kh)
    cs = jnp.clip(col - NA_KW // 2, 0, GRID_W - NA_KW)
    kr = rs[:, None, None] + jnp.arange(kh, dtype=jnp.int32)[None, :, None]
    kc = cs[:, None, None] + jnp.arange(NA_KW, dtype=jnp.int32)[None, None, :]
    shape = (L, kh, NA_KW)
    idx = jnp.broadcast_to(kr * GRID_W + kc, shape).reshape(L, kh * NA_KW)
    br = jnp.broadcast_to(kr - r[:, None, None] + (NA_KH - 1), shape).reshape(L, kh * NA_KW)
    bc = jnp.broadcast_to(kc - col[:, None, None] + (NA_KW - 1), shape).reshape(L, kh * NA_KW)
    return idx, br, bc


def _project(h, w_in):
    b, l, _ = h.shape
    q_da, k_da, v_da, q_na, k_na, v_na = jnp.split(h @ w_in, IN_SPLITS, axis=-1)
    return (q_da.reshape(b, l, DA_HEADS, 2, HEAD_DIM),
            k_da.reshape(b, l, DA_HEADS, 2, HEAD_DIM),
            v_da.reshape(b, l, DA_HEADS, 2 * HEAD_DIM),
            q_na.reshape(b, l, NA_HEADS, HEAD_DIM),
            k_na.reshape(b, l, NA_HEADS, HEAD_DIM),
            v_na.reshape(b, l, NA_HEADS, HEAD_DIM))


def _diff_weights(q, k, lam):
    s = jnp.einsum('bqhcd,bkhcd->bhcqk', q, k).astype(jnp.float32) * ATTN_SCALE
    p = jax.nn.softmax(s, axis=-1)
    return p[:, :, 0] - lam * p[:, :, 1]


def _diff_attention_latent(q, k_all, v_all, lam):
    def block(qb):
        w = _diff_weights(qb, k_all, lam).astype(v_all.dtype)
        return jnp.einsum('bhqk,bkhe->bqhe', w, v_all)
    return _from_blocks(lax.map(block, _to_blocks(q, Q_BLOCK)))


def _na_latent(q, k_lat, v_lat, k_ctx, v_ctx, rpb, idx, br, bc):
    nk = idx.shape[-1]
    nb = idx.shape[0] // Q_BLOCK

    def block(args):
        qb, ib, rb, cb = args
        kg = jnp.take(k_lat, ib, axis=1)
        vg = jnp.take(v_lat, ib, axis=1)
        s_loc = jnp.einsum('bqhd,bqnhd->bhqn', qb, kg).astype(jnp.float32) * ATTN_SCALE
        s_loc = s_loc + rpb[:, rb, cb].astype(jnp.float32)[None]
        s_ctx = jnp.einsum('bqhd,bkhd->bhqk', qb, k_ctx).astype(jnp.float32) * ATTN_SCALE
        p = jax.nn.softmax(jnp.concatenate([s_loc, s_ctx], axis=-1), axis=-1).astype(qb.dtype)
        return (jnp.einsum('bhqn,bqnhd->bqhd', p[..., :nk], vg)
                + jnp.einsum('bhqk,bkhd->bqhd', p[..., nk:], v_ctx))

    xs = (_to_blocks(q, Q_BLOCK), idx.reshape(nb, Q_BLOCK, nk),
          br.reshape(nb, Q_BLOCK, nk), bc.reshape(nb, Q_BLOCK, nk))
    return _from_blocks(lax.map(block, xs))


def _dense_attention(q, k, v):
    s = jnp.einsum('bqhd,bkhd->bhqk', q, k).astype(jnp.float32) * ATTN_SCALE
    p = jax.nn.softmax(s, axis=-1).astype(v.dtype)
    return jnp.einsum('bhqk,bkhd->bqhd', p, v)


def _mixer(h, hc, w_in, lam_vec, subln_g, rpb, w_o, lambda_init, cos, sin, nbr, with_ctx_out):
    b, l, _ = h.shape
    qd, kd, vd, qn, kn, vn = _project(h, w_in)
    qdc, kdc, vdc, qnc, knc, vnc = _project(hc, w_in)
    lv = lam_vec.astype(jnp.float32)
    lam = jnp.exp(jnp.sum(lv[0] * lv[1])) - jnp.exp(jnp.sum(lv[2] * lv[3])) + lambda_init
    qd = _apply_rope(qd, cos, sin)
    kd = _apply_rope(kd, cos, sin)
    k_all = jnp.concatenate([kdc, kd], axis=1)
    v_all = jnp.concatenate([vdc, vd], axis=1)
    od = _diff_attention_latent(qd, k_all, v_all, lam)
    od = (_rmsnorm(od, subln_g) * (1.0 - lambda_init)).reshape(b, l, DA_WIDTH)
    on = _na_latent(qn, kn, vn, knc, vnc, rpb, *nbr).reshape(b, l, NA_WIDTH)
    y = jnp.concatenate([od, on], axis=-1) @ w_o
    if not with_ctx_out:
        return y, None
    lc = hc.shape[1]
    wdc = _diff_weights(qdc, kdc, lam).astype(vdc.dtype)
    odc = jnp.einsum('bhqk,bkhe->bqhe', wdc, vdc)
    odc = (_rmsnorm(odc, subln_g) * (1.0 - lambda_init)).reshape(b, lc, DA_WIDTH)
    onc = _dense_attention(qnc, knc, vnc).reshape(b, lc, NA_WIDTH)
    yc = jnp.concatenate([odc, onc], axis=-1) @ w_o
    return y, yc


def _dwconv3(x, w, bias):
    xp = jnp.pad(x, ((0, 0), (1, 1), (0, 0)))
    return xp[:, :-2] * w[0] + xp[:, 1:-1] * w[1] + xp[:, 2:] * w[2] + bias


def _conv_ffn(h, w_up, conv_w, conv_b, w_down):
    g, u = jnp.split(h @ w_up, 2, axis=-1)
    g = _dwconv3(g, conv_w, conv_b)
    return (jax.nn.silu(g) * u) @ w_down


def setup_inputs(seed: int = 0) -> dict:
    key = jax.random.key(seed)
    ks = jax.random.split(key, 24)
    beta = (8.0 * DEPTH) ** -0.25
    f32 = jnp.float32

    def nrm(k, shape, scale):
        return jax.random.normal(k, shape, f32) * scale

    return {
        'x': nrm(ks[0], (BATCH, SEQ, D_MODEL), 1.0),
        'c': nrm(ks[1], (BATCH, D_MODEL), 1.0),
        'ctx': nrm(ks[2], (BATCH, CTX_LEN, D_MODEL), 1.0),
        'c_ctx': nrm(ks[3], (D_MODEL,), 1.0),
        'w_ada': nrm(ks[4], (DEPTH, D_MODEL, N_MOD * D_MODEL), 0.25 * D_MODEL ** -0.5),
        'b_ada': nrm(ks[5], (DEPTH, N_MOD * D_MODEL), 0.01),
        'w_in': nrm(ks[6], (DEPTH, D_MODEL, IN_DIM), D_MODEL ** -0.5),
        'da_lambda': nrm(ks[7], (DEPTH, 4, HEAD_DIM), 0.1),
        'da_subln': 1.0 + nrm(ks[8], (DEPTH, 2 * HEAD_DIM), 0.02),
        'na_rpb': nrm(ks[9], (DEPTH, NA_HEADS, 2 * NA_KH - 1, 2 * NA_KW - 1), 0.1),
        'w_o': nrm(ks[10], (DEPTH, MIX_DIM, D_MODEL), beta * MIX_DIM ** -0.5),
        'ln1_g': 1.0 + nrm(ks[11], (DEPTH, D_MODEL), 0.02),
        'ln1_b': nrm(ks[12], (DEPTH, D_MODEL), 0.02),
        'w_up': nrm(ks[13], (DEPTH, D_MODEL, 2 * D_FF), D_MODEL ** -0.5),
        'conv_w': nrm(ks[14], (DEPTH, CONV_W, D_FF), CONV_W ** -0.5),
        'conv_b': nrm(ks[15], (DEPTH, D_FF), 0.02),
        'w_down': nrm(ks[16], (DEPTH, D_FF, D_MODEL), beta * D_FF ** -0.5),
        'ln2_g': 1.0 + nrm(ks[17], (DEPTH, D_MODEL), 0.02),
        'ln2_b': nrm(ks[18], (DEPTH, D_MODEL), 0.02),
    }


def reference(x, c, ctx, c_ctx, w_ada, b_ada, w_in, da_lambda, da_subln, na_rpb, w_o,
              ln1_g, ln1_b, w_up, conv_w, conv_b, w_down, ln2_g, ln2_b):
    L = x.shape[1]
    alpha = (2.0 * DEPTH) ** 0.25
    cos, sin = _axial_rope_tables(L)
    nbr = _na_neighbours(L)
    xc = ctx
    for l in range(DEPTH):
        last = l == DEPTH - 1
        lambda_init = 0.8 - 0.6 * math.exp(-0.3 * l)
        mod = (jax.nn.silu(c) @ w_ada[l] + b_ada[l])[:, None, :]
        mod_c = jax.nn.silu(c_ctx) @ w_ada[l] + b_ada[l]
        sh_a, sc_a, g_a, sh_m, sc_m, g_m = jnp.split(mod, N_MOD, axis=-1)
        shc_a, scc_a, gc_a, shc_m, scc_m, gc_m = jnp.split(mod_c, N_MOD, axis=-1)
        y, yc = _mixer(x * (1.0 + sc_a) + sh_a, xc * (1.0 + scc_a) + shc_a,
                       w_in[l], da_lambda[l], da_subln[l], na_rpb[l], w_o[l],
                       lambda_init, cos, sin, nbr, not last)
        x = _layernorm(alpha * x + g_a * y, ln1_g[l], ln1_b[l])
        f = _conv_ffn(x * (1.0 + sc_m) + sh_m, w_up[l], conv_w[l], conv_b[l], w_down[l])
        x = _layernorm(alpha * x + g_m * f, ln2_g[l], ln2_b[l])
        if not last:
            xc = _layernorm(alpha * xc + gc_a * yc, ln1_g[l], ln1_b[l])
            fc = _conv_ffn(xc * (1.0 + scc_m) + shc_m, w_up[l], conv_w[l], conv_b[l], w_down[l])
            xc = _layernorm(alpha * xc + gc_m * fc, ln2_g[l], ln2_b[l])
    return x
```

```python
import contextlib
import math
import numpy as np
import concourse.bass as bass
import concourse.mybir as mybir
from concourse.bass_utils import run_bass_kernel_spmd

F32 = mybir.dt.float32
BF16 = mybir.dt.bfloat16
ALU = mybir.AluOpType
AF = mybir.ActivationFunctionType
AX = mybir.AxisListType

NCORES = 8
D = 2048
SEQ = 16384
DEPTH = 2
TOK = SEQ // NCORES
CTX = 256
NT = TOK + CTX
GRID_W = 64
HD = 128
DFF = 5632
NFC = DFF // 128
IN_DIM = 6144
LN_EPS = 1e-5
SCALE = HD ** -0.5
ALPHA = (2.0 * DEPTH) ** 0.25
NEG = -30000.0
GROUPS = [(0, 410), (410, 410), (820, 410), (1230, 410), (1640, 408), (TOK, CTX)]
H2W = NT
NKC = 2 + NCORES * 16


class Buf:
    __slots__ = ("w", "r", "pr", "name")

    def __init__(self, name=""):
        self.w = {}
        self.r = {}
        self.pr = {}
        self.name = name


def _merge(dst, src):
    for k, v in src.items():
        if dst.get(k, 0) < v:
            dst[k] = v


class Trk:
    NS = 16

    def __init__(self, nc, es):
        self.nc = nc
        self.eng = dict(pe=nc.tensor, act=nc.scalar, dve=nc.vector, pool=nc.gpsimd, sp=nc.sync)
        self.sem = {e: es.enter_context(nc.semaphore("s_" + e)) for e in ("pe", "act", "dve", "pool")}
        self.cnt = {e: 0 for e in self.sem}
        self.waited = {e: {} for e in self.eng}
        self.dsem = {}
        self.dval = {}
        self.dnext = {}
        for q in ("sp", "pool"):
            self.dsem[q] = [es.enter_context(nc.semaphore("d_%s%d" % (q, i))) for i in range(self.NS)]
            self.dval[q] = [0] * self.NS
            self.dnext[q] = 0
        self.ccsem = es.enter_context(nc.semaphore("s_cc"))
        self.ccval = 0
        self.semobj = {}
        for e, s in self.sem.items():
            self.semobj[e] = s
        for q in self.dsem:
            for i, s in enumerate(self.dsem[q]):
                self.semobj[("d", q, i)] = s
        self.semobj["cc"] = self.ccsem

    def _wait(self, x, key, val):
        if val <= 0:
            return
        if key == "pe" and x == "pe":
            return
        if self.waited[x].get(key, 0) >= val:
            return
        self.eng[x].wait_ge(self.semobj[key], val)
        self.waited[x][key] = val

    def _deps(self, x, reads, writes, partial):
        deps = {}
        for b in reads:
            _merge(deps, b.w)
        for b in writes:
            _merge(deps, b.w)
            _merge(deps, b.r)
            _merge(deps, b.pr)
        for b in partial:
            _merge(deps, b.r)
            _merge(deps, b.pr)
        for k, v in deps.items():
            self._wait(x, k, v)

    def _commit(self, key, val, reads, writes, partial):
        for b in reads:
            if b.r.get(key, 0) < val:
                b.r[key] = val
        for b in writes:
            pr = {}
            _merge(pr, b.r)
            _merge(pr, b.w)
            b.pr = pr
            b.r = {}
            b.w = {key: val}
        for b in partial:
            if b.w.get(key, 0) < val:
                b.w[key] = val

    def op(self, x, fn, reads=(), writes=(), partial=()):
        self._deps(x, reads, writes, partial)
        ins = fn(self.eng[x])
        self.cnt[x] += 1
        ins.then_inc(self.sem[x], 1)
        self._commit(x, self.cnt[x], reads, writes, partial)
        return ins

    def dma(self, q, out, in_, reads=(), writes=(), partial=(), **kw):
        i = self.dnext[q]
        self.dnext[q] = (i + 1) % self.NS
        key = ("d", q, i)
        self._wait(q, key, self.dval[q][i])
        self._deps(q, reads, writes, partial)
        self.eng[q].dma_start(out=out, in_=in_, **kw).then_inc(self.dsem[q][i], 16)
        self.dval[q][i] += 16
        self._commit(key, self.dval[q][i], reads, writes, partial)

    def allgather(self, in_ap, out_ap, reads=(), writes=()):
        self._deps("pool", reads, writes, ())
        self.eng["pool"].collective_compute(
            "AllGather", ALU.bypass, replica_groups=[list(range(NCORES))],
            ins=[in_ap], outs=[out_ap]).then_inc(self.ccsem, 1)
        self.ccval += 1
        self._commit("cc", self.ccval, reads, writes, ())

    def barrier(self):
        tot = dict(self.cnt)
        for q in self.dsem:
            for i in range(self.NS):
                tot[("d", q, i)] = self.dval[q][i]
        tot["cc"] = self.ccval
        for x in self.eng:
            for k, v in tot.items():
                if k == x:
                    continue
                self._wait(x, k, v)

    def finish(self, x="sp"):
        tot = dict(self.cnt)
        for q in self.dsem:
            for i in range(self.NS):
                tot[("d", q, i)] = self.dval[q][i]
        tot["cc"] = self.ccval
        for k, v in tot.items():
            if k == "pe" and x == "pe":
                continue
            if v > 0 and self.waited[x].get(k, 0) < v:
                self.eng[x].wait_ge(self.semobj[k], v)
                self.waited[x][k] = v


def _subtiles(n, step=128):
    return [(s, min(step, n - s)) for s in range(0, n, step)]


def build_nc(n_layers=DEPTH, dbg=None):
    nc = bass.Bass("TRN2", target_bir_lowering=False)
    es = contextlib.ExitStack()

    def din(name, shape, dt=F32):
        return nc.dram_tensor(name, list(shape), dt, kind="ExternalInput").ap()

    x_in = din("x", [TOK, D])
    ctx_in = din("ctx", [CTX, D])
    cT_in = din("cT", [128, 16, 2])
    w_ada = din("w_ada", [DEPTH, D, 6 * D // NCORES])
    b_adaT = din("b_adaT", [DEPTH, 128, 96])
    w_in_p = din("w_in", [DEPTH, D // NCORES, IN_DIM])
    lam_in = din("lam_bc", [DEPTH, 128, 512])
    subln_in = din("subln_bc", [DEPTH, 128, 256])
    w_o_p = din("w_o", [DEPTH, D // NCORES, D])
    lnp_in = din("lnp", [DEPTH, 128, 4, 16])
    w_up_p = din("w_up", [DEPTH, D // NCORES, 2 * DFF])
    convp_in = din("convp", [DEPTH, 128, 4, NFC])
    w_down_p = din("w_down", [DEPTH, DFF // NCORES, D])
    rope_in = din("ropeT", [2, 128, TOK])
    perm_in = din("perm", [128, 128])
    ident_in = din("ident", [128, 128])
    nab_in = din("na_bias", [DEPTH, 8, 128, 7, 128])
    nam_in = din("na_mask", [5, 128, 7, 128])
    sel_in = din("sel", [128, 16])
    out = nc.dram_tensor("out", [TOK, D], F32, kind="ExternalOutput").ap()
    dbg_out = None

    xT = nc.dram_tensor("xT", [16, 128, NT], F32).ap()
    qT = nc.dram_tensor("qT", [16, 128, NT], BF16).ap()
    kv = nc.dram_tensor("kv", [4096, TOK], BF16)
    kvg = nc.dram_tensor("kvg", [NCORES * 4096, TOK], BF16)
    kcT = nc.dram_tensor("kcT", [16, 128, CTX], BF16).ap()
    vc = nc.dram_tensor("vc", [CTX, D], BF16).ap()
    oT = nc.dram_tensor("oT", [16, 128, NT], BF16).ap()
    h2T = nc.dram_tensor("h2T", [16, 128, H2W], BF16).ap()
    hb = nc.dram_tensor("hb", [128, 32], BF16)
    hbg = nc.dram_tensor("hbg", [NCORES * 128, 32], BF16)
    kva = kv.ap()
    WSPEC = dict(w_in=(w_in_p, D, IN_DIM), w_o=(w_o_p, D, D), w_up=(w_up_p, D, 2 * DFF), w_down=(w_down_p, DFF, D))
    wl_t = {k: [nc.dram_tensor("wl_%s%d" % (k, l), [v[1] // NCORES, v[2]], BF16) for l in range(DEPTH)] for k, v in WSPEC.items()}
    wg_t = {k: [nc.dram_tensor("wg_%s%d" % (k, l), [v[1], v[2]], BF16) for l in range(DEPTH)] for k, v in WSPEC.items()}
    w_in = [wg_t["w_in"][l].ap() for l in range(DEPTH)]
    w_o = [wg_t["w_o"][l].ap() for l in range(DEPTH)]
    w_up = [wg_t["w_up"][l].ap() for l in range(DEPTH)]
    w_down = [wg_t["w_down"][l].ap() for l in range(DEPTH)]
    mp = nc.dram_tensor("mp", [128, DEPTH * 12 * 2], F32)
    mg = nc.dram_tensor("mg", [NCORES * 128, DEPTH * 12 * 2], F32)
    kvga = kvg.ap()

    t = Trk(nc, es)
    B = Buf
    b_xT = [B() for _ in GROUPS]
    b_qT = B()
    b_kv = B()
    b_kvg = B()
    b_kc = B()
    b_oT = B()
    b_h2T = B()
    b_hb = B()
    b_hbg = B()
    b_out = B()
    b_dbg = B()

    _uc = [0]

    def ust(name, shape, dt):
        _uc[0] += 1
        return nc.sbuf_tensor("%s_u%d" % (name, _uc[0]), list(shape), dt)

    def sb(name, shape, dt):
        return es.enter_context(nc.sbuf_tensor("s_" + name, list(shape), dt))

    ident = sb("ident", [128, 128], F32)
    perm = sb("perm", [128, 128], F32)
    onesm = sb("onesm", [128, 128], F32)
    modT = sb("modT", [128, DEPTH, 96, 2], F32)
    prm = sb("prm", [128, DEPTH, 8, 16, 2], F32)
    lnp = sb("lnp", [128, DEPTH, 4, 16], F32)
    convp = sb("convp", [128, DEPTH, 4, NFC], F32)
    selb = sb("selb", [128, 16], F32)
    selI = sb("selI", [128, 16, 128], BF16)
    hh = sb("hh", [128, 2, 16], BF16)
    b_hh = B()
    epsT = sb("epsT", [128, 1], F32)
    b_const = B()
    b_mod = B()
    b_prm = B()

    ps = [es.enter_context(nc.psum_tensor("ps%d" % i, [128, 512], F32)) for i in range(8)]
    b_ps = [B("ps%d" % i) for i in range(8)]

    def load_consts():
        t.dma("sp", ident[:], ident_in[:, :], writes=[b_const])
        t.dma("sp", perm[:], perm_in[:, :], partial=[b_const])
        t.dma("sp", lnp[:], lnp_in.rearrange("l p a j -> p l a j"), partial=[b_const])
        t.dma("sp", convp[:], convp_in.rearrange("l p a j -> p l a j"), partial=[b_const])
        t.dma("sp", selb[:], sel_in[:, :], partial=[b_const])
        t.op("dve", lambda e: e.memset(onesm[:], 1.0 / D), partial=[b_const])
        t.op("dve", lambda e: e.memset(epsT[:], LN_EPS), partial=[b_const])
        for r in range(16):
            t.op("dve", lambda e, r=r: e.tensor_scalar(
                out=selI[:, r, :], in0=ident[:], scalar1=selb[:, r:r + 1], scalar2=None, op0=ALU.mult),
                reads=[b_const], partial=[b_const])

    b_wg = {k: [B() for _ in range(DEPTH)] for k in ("w_in", "w_o", "w_up", "w_down")}

    def stage_w():
        with contextlib.ExitStack() as ls:
            stg = [ls.enter_context(ust("wstg%d" % i, [128, 2 * 2 * DFF], BF16)) for i in range(2)]
            b_stg = [B(), B()]
            b_wl = B()
            i = 0
            for l in range(DEPTH):
                for k in ("w_in", "w_o", "w_up", "w_down"):
                    src, rows, cols = WSPEC[k]
                    rp = rows // NCORES
                    npart = 128 if rp % 128 == 0 else 64
                    na = rp // npart
                    st, bs = stg[i % 2], b_stg[i % 2]
                    i += 1
                    view = st[0:npart, 0:na * cols].rearrange("p (a n) -> p a n", n=cols)
                    t.dma("pool", view, src[l, :, :].rearrange("(a p) n -> p a n", p=npart), writes=[bs])
                    t.dma("sp", wl_t[k][l].ap().rearrange("(a p) n -> p a n", p=npart), view, reads=[bs], writes=[b_wl])
                    t.allgather(wl_t[k][l].ap().opt(), wg_t[k][l].ap().opt(), reads=[b_wl], writes=[b_wg[k][l]])
            t.barrier()

    def stage_mod():
        with contextlib.ExitStack() as ls:
            sc = ls.enter_context(ust("sc", [128, 16, 2], F32))
            bT = ls.enter_context(ust("bT", [128, DEPTH, 96], F32))
            wsl = [ls.enter_context(ust("wada%d" % i, [128, 16, 512], F32)) for i in range(2)]
            b_sc = B()
            b_w = [B(), B()]
            t.dma("sp", sc[:], cT_in[:, :, :], writes=[b_sc])
            t.dma("sp", bT[:], b_adaT.rearrange("l p j -> p l j"), partial=[b_sc])
            t.op("act", lambda e: e.activation(out=sc[:], in_=sc[:], func=AF.Silu), reads=[b_sc], writes=[b_sc])
            it = 0
            mps = ls.enter_context(ust("mps", [128, DEPTH * 12 * 2], F32))
            b_mps, b_mp, b_mg = B(), B(), B()
            for l in range(DEPTH):
                for nb in range(3):
                    w = wsl[it % 2]
                    bw = b_w[it % 2]
                    it += 1
                    t.dma("sp", w[:], w_ada[l, :, nb * 512:(nb + 1) * 512].rearrange("(k p) n -> p k n", p=128),
                          writes=[bw])
                    for s in range(4):
                        j = l * 12 + nb * 4 + s
                        for k in range(16):
                            t.op("pe", lambda e, k=k, s=s, j=j, w=w: e.matmul(
                                ps[0][:, 2 * j:2 * j + 2], lhsT=w[:, k, s * 128:(s + 1) * 128], rhs=sc[:, k, :],
                                start=(k == 0), stop=(k == 15)),
                                reads=[bw, b_sc],
                                writes=([b_ps[0]] if (j == 0 and k == 0) else []),
                                partial=([] if (j == 0 and k == 0) else [b_ps[0]]))
            t.op("dve", lambda e: e.tensor_copy(out=mps[:], in_=ps[0][:, 0:DEPTH * 24]), reads=[b_ps[0]], writes=[b_mps])
            t.dma("sp", mp.ap()[:, :], mps[:], reads=[b_mps], writes=[b_mp])
            t.allgather(mp.ap().opt(), mg.ap().opt(), reads=[b_mp], writes=[b_mg])
            for l in range(DEPTH):
                t.dma("sp", modT[:, l, :, :].rearrange("p (i jj) t -> p i (jj t)", i=NCORES),
                      mg.ap().rearrange("(i p) (l x) -> p l i x", i=NCORES, l=DEPTH)[:, l, :, :],
                      reads=[b_mg], writes=([b_mod] if l == 0 else []), partial=([] if l == 0 else [b_mod]))
            for l in range(DEPTH):
                for tt in range(2):
                    t.op("dve", lambda e, tt=tt, l=l: e.tensor_tensor(
                        out=modT[:, l, :, tt], in0=modT[:, l, :, tt], in1=bT[:, l, :], op=ALU.add),
                        reads=[b_mod, b_sc], partial=[b_mod])
            for l in range(n_layers):
                def m(which, l=l):
                    return modT[:, l, which * 16:(which + 1) * 16, :]
                first = (l == 0)
                def dv(fn, first=False):
                    t.op("dve", fn, reads=[b_mod, b_const, b_prm], writes=[b_prm])
                dv(lambda e, l=l: e.tensor_scalar(out=prm[:, l, 0], in0=m(1), scalar1=1.0, scalar2=None, op0=ALU.add), first)
                dv(lambda e, l=l: e.tensor_copy(out=prm[:, l, 1], in_=m(0)))
                dv(lambda e, l=l: e.tensor_copy(out=prm[:, l, 2], in_=m(2)))
                dv(lambda e, l=l: e.tensor_copy(out=prm[:, l, 5], in_=m(5)))
                dv(lambda e, l=l: e.tensor_scalar(out=prm[:, l, 6], in0=m(4), scalar1=1.0, scalar2=None, op0=ALU.add))
                for tt in range(2):
                    dv(lambda e, l=l, tt=tt: e.tensor_tensor(out=prm[:, l, 3, :, tt], in0=prm[:, l, 6, :, tt],
                                                            in1=lnp[:, l, 0, :], op=ALU.mult))
                    dv(lambda e, l=l, tt=tt: e.tensor_tensor(out=prm[:, l, 4, :, tt], in0=prm[:, l, 6, :, tt],
                                                            in1=lnp[:, l, 1, :], op=ALU.mult))
                    dv(lambda e, l=l, tt=tt: e.tensor_tensor(out=prm[:, l, 4, :, tt], in0=prm[:, l, 4, :, tt],
                                                            in1=modT[:, l, 48:64, tt], op=ALU.add))
            t.barrier()

    def stage_in():
        with contextlib.ExitStack() as ls:
            xs = [ls.enter_context(ust("xin%d" % i, [128, D], F32)) for i in range(2)]
            xo = [ls.enter_context(ust("xo%d" % i, [128, 16, 128], F32)) for i in range(2)]
            b_xs = [B(), B()]
            b_xo = [B(), B()]
            for ti in range(NT // 128):
                src = x_in[ti * 128:(ti + 1) * 128, :] if ti < 16 else ctx_in[(ti - 16) * 128:(ti - 15) * 128, :]
                xs_, xo_ = xs[ti % 2], xo[ti % 2]
                t.dma("sp", xs_[:], src, writes=[b_xs[ti % 2]])
                for jb in range(4):
                    pb = (ti * 4 + jb) % 8
                    for s in range(4):
                        j = jb * 4 + s
                        t.op("pe", lambda e, j=j, s=s, pb=pb, xs_=xs_: e.transpose(
                            ps[pb][:, s * 128:(s + 1) * 128], xs_[:, j * 128:(j + 1) * 128], ident[:]),
                            reads=[b_xs[ti % 2], b_const], writes=([b_ps[pb]] if s == 0 else []),
                            partial=([] if s == 0 else [b_ps[pb]]))
                    eng = "act" if jb % 2 else "dve"
                    if eng == "act":
                        fn = lambda e, jb=jb, pb=pb, xo_=xo_: e.copy(
                            out=xo_[:, jb * 4:(jb + 1) * 4, :], in_=ps[pb][:].rearrange("p (s t) -> p s t", t=128))
                    else:
                        fn = lambda e, jb=jb, pb=pb, xo_=xo_: e.tensor_copy(
                            out=xo_[:, jb * 4:(jb + 1) * 4, :], in_=ps[pb][:].rearrange("p (s t) -> p s t", t=128))
                    t.op(eng, fn, reads=[b_ps[pb]], writes=([b_xo[ti % 2]] if jb == 0 else []),
                         partial=([] if jb == 0 else [b_xo[ti % 2]]))
                gi = [g for g, (s0, n) in enumerate(GROUPS) if s0 < (ti + 1) * 128 and ti * 128 < s0 + n]
                t.dma("sp", xT[:, :, ti * 128:(ti + 1) * 128].rearrange("j p t -> p j t"), xo_[:],
                      reads=[b_xo[ti % 2]], partial=[b_xT[g] for g in gi])
            t.barrier()

    def stage_qkv(l, last):
        with contextlib.ExitStack() as ls:
            xg = [ls.enter_context(ust("xg%d" % i, [128, 16, 410], F32)) for i in range(1)]
            hT = ls.enter_context(ust("hT", [128, 16, 410], BF16))
            ws = [ls.enter_context(ust("win%d" % i, [128, 16, 512], BF16)) for i in range(2)]
            rope = ls.enter_context(ust("rope", [128, 2, 410], F32))
            qf = [ls.enter_context(ust("qf%d" % i, [128, 410], F32)) for i in range(2)]
            t1 = [ls.enter_context(ust("t1_%d" % i, [128, 410], F32)) for i in range(2)]
            ob = [ls.enter_context(ust("ob%d" % i, [128, 512], BF16)) for i in range(4)]
            b_xg, b_hT, b_rope = B(), B(), B()
            b_ws = [B(), B()]
            b_qf = [B(), B()]
            b_t1 = [B(), B()]
            b_ob = [B() for _ in range(4)]
            wi = 0
            oi = 0
            qi = 0
            pi = 0
            first_kv = True
            first_q = True
            first_kc = True
            for g, (s0, n) in enumerate(GROUPS):
                isctx = (g == len(GROUPS) - 1)
                tt = 1 if isctx else 0
                t.dma("sp", xg[0][:, :, 0:n], xT[:, :, s0:s0 + n].rearrange("j p t -> p j t"),
                      reads=[b_xT[g]], writes=[b_xg])
                if not isctx:
                    t.dma("sp", rope[:, :, 0:n], rope_in[:, :, s0:s0 + n].rearrange("a p t -> p a t"), writes=[b_rope])
                for j in range(16):
                    t.op("dve" if j % 2 else "pool", lambda e, j=j, n=n, tt=tt: e.tensor_scalar(
                        out=hT[:, j, 0:n], in0=xg[0][:, j, 0:n], scalar1=prm[:, l, 0, j, tt:tt + 1],
                        scalar2=prm[:, l, 1, j, tt:tt + 1], op0=ALU.mult, op1=ALU.add),
                        reads=[b_xg, b_prm], writes=([b_hT] if j == 0 else []), partial=([] if j == 0 else [b_hT]))
                for nb in range(12):
                    if isctx and last and nb in (0, 1, 6, 7):
                        continue
                    w = ws[wi % 2]
                    bw = b_ws[wi % 2]
                    wi += 1
                    t.dma("sp", w[:], w_in[l][:, nb * 512:(nb + 1) * 512].rearrange("(k p) n -> p k n", p=128),
                          reads=[b_wg["w_in"][l]], writes=[bw])
                    if nb in (4, 5, 10, 11):
                        cbase = (nb - 4) * 512 if nb < 6 else 1024 + (nb - 10) * 512
                        for (ts, tn) in _subtiles(n):
                            pb = pi % 4
                            pi += 1
                            for k in range(16):
                                t.op("pe", lambda e, k=k, ts=ts, tn=tn, pb=pb, w=w: e.matmul(
                                    ps[pb][0:tn, :], lhsT=hT[:, k, ts:ts + tn], rhs=w[:, k, :],
                                    start=(k == 0), stop=(k == 15)),
                                    reads=[bw, b_hT], writes=([b_ps[pb]] if k == 0 else []),
                                    partial=([] if k == 0 else [b_ps[pb]]))
                            o = ob[oi % 4]
                            bo = b_ob[oi % 4]
                            oi += 1
                            t.op("act", lambda e, o=o, pb=pb, tn=tn: e.copy(out=o[0:tn, :], in_=ps[pb][0:tn, :]),
                                 reads=[b_ps[pb]], writes=[bo])
                            if isctx:
                                t.dma("sp", vc[ts:ts + tn, cbase:cbase + 512], o[0:tn, :], reads=[bo],
                                      writes=([b_kc] if first_kc else []), partial=([] if first_kc else [b_kc]))
                                first_kc = False
                            else:
                                t.dma("sp", kva[2048 + s0 + ts:2048 + s0 + ts + tn, cbase:cbase + 512], o[0:tn, :],
                                      reads=[bo], writes=([b_kv] if first_kv else []),
                                      partial=([] if first_kv else [b_kv]))
                                first_kv = False
                        continue
                    for s in range(4):
                        isq = nb in (0, 1, 6, 7)
                        isda = nb < 4
                        ch = (nb % 2) * 4 + s + (0 if isda else 8)
                        pb = pi % 4
                        pi += 1
                        for k in range(16):
                            t.op("pe", lambda e, k=k, s=s, n=n, pb=pb, w=w: e.matmul(
                                ps[pb][:, 0:n], lhsT=w[:, k, s * 128:(s + 1) * 128], rhs=hT[:, k, 0:n],
                                start=(k == 0), stop=(k == 15)),
                                reads=[bw, b_hT], writes=([b_ps[pb]] if k == 0 else []),
                                partial=([] if k == 0 else [b_ps[pb]]))
                        o = ob[oi % 4]
                        bo = b_ob[oi % 4]
                        oi += 1
                        sc_ = SCALE if isq else 1.0
                        if isda and not isctx:
                            q_ = qf[qi % 2]
                            bq = b_qf[qi % 2]
                            t_ = t1[qi % 2]
                            bt = b_t1[qi % 2]
                            qi += 1
                            pr = 4 + (pi % 4)
                            t.op("act", lambda e, q_=q_, pb=pb, n=n, sc_=sc_: e.mul(out=q_[:, 0:n], in_=ps[pb][:, 0:n], mul=sc_),
                                 reads=[b_ps[pb]], writes=[bq])
                            t.op("pe", lambda e, q_=q_, pr=pr, n=n: e.matmul(
                                ps[pr][:, 0:n], lhsT=perm[:], rhs=q_[:, 0:n], start=True, stop=True),
                                reads=[bq, b_const], writes=[b_ps[pr]])
                            t.op("dve", lambda e, q_=q_, t_=t_, n=n: e.tensor_tensor(
                                out=t_[:, 0:n], in0=q_[:, 0:n], in1=rope[:, 0, 0:n], op=ALU.mult),
                                reads=[bq, b_rope], writes=[bt])
                            t.op("dve", lambda e, q_=q_, pr=pr, n=n: e.tensor_tensor(
                                out=q_[:, 0:n], in0=ps[pr][:, 0:n], in1=rope[:, 1, 0:n], op=ALU.mult),
                                reads=[b_ps[pr], b_rope, bq], writes=[bq])
                            t.op("dve", lambda e, q_=q_, t_=t_, o=o, n=n: e.tensor_tensor(
                                out=o[:, 0:n], in0=q_[:, 0:n], in1=t_[:, 0:n], op=ALU.add),
                                reads=[bq, bt], writes=[bo])
                        else:
                            t.op("act", lambda e, o=o, pb=pb, n=n, sc_=sc_: e.mul(out=o[:, 0:n], in_=ps[pb][:, 0:n], mul=sc_),
                                 reads=[b_ps[pb]], writes=[bo])
                        if isq:
                            t.dma("sp", qT[ch, :, s0:s0 + n], o[:, 0:n], reads=[bo],
                                  writes=([b_qT] if first_q else []), partial=([] if first_q else [b_qT]))
                            first_q = False
                        elif isctx:
                            t.dma("sp", kcT[ch, :, :], o[:, 0:n], reads=[bo],
                                  writes=([b_kc] if first_kc else []), partial=([] if first_kc else [b_kc]))
                            first_kc = False
                        else:
                            t.dma("sp", kva[ch * 128:(ch + 1) * 128, s0:s0 + n], o[:, 0:n], reads=[bo],
                                  writes=([b_kv] if first_kv else []), partial=([] if first_kv else [b_kv]))
                            first_kv = False
            t.allgather(kva.opt(), kvga.opt(), reads=[b_kv], writes=[b_kvg])
            t.barrier()

    def emit_outT(src_f32, ncols, tok0, ntok, chunk0, trg, bufs):
        (pb, b_pb, ot, b_ot) = bufs
        nch = ncols // 128
        for c in range(nch):
            t.op("pe", lambda e, c=c: e.transpose(ps[pb][:, c * 128:c * 128 + ntok], src_f32[:, c * 128:(c + 1) * 128], ident[0:ntok, 0:ntok]),
                 reads=[trg, b_const], writes=([b_pb] if c == 0 else []), partial=([] if c == 0 else [b_pb]))
        t.op("act", lambda e: e.copy(out=ot[:, 0:nch, 0:ntok], in_=ps[pb][:, 0:nch * 128].rearrange("p (c t) -> p c t", t=128)[:, :, 0:ntok]),
             reads=[b_pb], writes=[b_ot])
        t.dma("sp", oT[chunk0:chunk0 + nch, :, tok0:tok0 + ntok].rearrange("c p t -> p c t"), ot[:, 0:nch, 0:ntok],
              reads=[b_ot], partial=[b_oT])

    def stage_attn(l, last):
        lam_init = 0.8 - 0.6 * math.exp(-0.3 * l)
        with contextlib.ExitStack() as ls:
            KT = [ls.enter_context(ust("KT%d" % i, [128, NKC * 128], BF16)) for i in range(2)]
            V = ls.enter_context(ust("Vda", [128, NKC, 257], BF16))
            QT = [ls.enter_context(ust("QT%d" % i, [128, NT], BF16)) for i in range(2)]
            PT = [ls.enter_context(ust("PT%d" % i, [128, 512], BF16)) for i in range(4)]
            SBANK = [0, 1, 7]
            O1 = ls.enter_context(ust("O1n", [128, NT // 128, 256], F32))
            od = [ls.enter_context(ust("od%d" % i, [128, 256], F32)) for i in range(4)]
            sq = ls.enter_context(ust("sq", [128, 256], F32))
            sm = [ls.enter_context(ust("sm%d" % i, [128, 4], F32)) for i in range(2)]
            otb = [ls.enter_context(ust("otb%d" % i, [128, 2, 128], BF16)) for i in range(4)]
            lamt = ls.enter_context(ust("lamt", [128, 512], F32))
            lams = ls.enter_context(ust("lams", [128, 8], F32))
            gsub = ls.enter_context(ust("gsub", [128, 256], F32))
            b_KT = [B(), B()]
            b_V, b_lam = B(), B()
            b_QT = [B(), B()]
            b_PT = [B(), B(), B(), B()]
            b_O1 = B()
            b_od = [B() for _ in range(4)]
            b_sq = B()
            b_sm = [B(), B()]
            b_otb = [B() for _ in range(4)]
            deferred = []
            t.dma("sp", lamt[:], lam_in[l, :, :], writes=[b_lam])
            t.dma("sp", gsub[:], subln_in[l, :, :], partial=[b_lam])
            t.op("dve", lambda e: e.tensor_tensor(out=lamt[:, 0:128], in0=lamt[:, 0:128], in1=lamt[:, 128:256], op=ALU.mult),
                 reads=[b_lam], writes=[b_lam])
            t.op("dve", lambda e: e.tensor_tensor(out=lamt[:, 256:384], in0=lamt[:, 256:384], in1=lamt[:, 384:512], op=ALU.mult),
                 reads=[b_lam], writes=[b_lam])
            t.op("dve", lambda e: e.reduce_sum(out=lams[:, 0:1], in_=lamt[:, 0:128], axis=AX.X), reads=[b_lam], writes=[b_lam])
            t.op("dve", lambda e: e.reduce_sum(out=lams[:, 1:2], in_=lamt[:, 256:384], axis=AX.X), reads=[b_lam], writes=[b_lam])
            t.op("act", lambda e: e.activation(out=lams[:, 2:4], in_=lams[:, 0:2], func=AF.Exp), reads=[b_lam], writes=[b_lam])
            t.op("dve", lambda e: e.tensor_tensor(out=lams[:, 4:5], in0=lams[:, 3:4], in1=lams[:, 2:3], op=ALU.subtract),
                 reads=[b_lam], writes=[b_lam])
            t.op("dve", lambda e: e.tensor_scalar(out=lams[:, 5:6], in0=lams[:, 4:5], scalar1=-lam_init, scalar2=None, op0=ALU.add),
                 reads=[b_lam], writes=[b_lam])
            t.op("dve", lambda e: e.tensor_scalar(out=gsub[:], in0=gsub[:], scalar1=1.0 - lam_init, scalar2=None, op0=ALU.mult),
                 reads=[b_lam], writes=[b_lam])
            t.op("pool", lambda e: e.memset(V[:, :, 256:257], 1.0), writes=[b_V])
            si = 0
            pti = 0
            odi = 0
            for h in range(4):
                t.dma("sp", V[:, 0:2, 0:256], vc[:, h * 256:(h + 1) * 256].rearrange("(c p) e -> p c e", p=128),
                      reads=[b_kc], writes=[b_V])
                for r in range(NCORES):
                    t.dma("sp", V[:, 2 + r * 16:2 + (r + 1) * 16, 0:256],
                          kvga[r * 4096 + 2048:(r + 1) * 4096, h * 256:(h + 1) * 256].rearrange("(c p) e -> p c e", p=128),
                          reads=[b_kvg], partial=[b_V])
                for c in range(2):
                    hm = h * 2 + c
                    KT_, bK = KT[hm % 2], b_KT[hm % 2]
                    QT_, bQ = QT[hm % 2], b_QT[hm % 2]
                    t.dma("sp", KT_[:, 0:CTX], kcT[hm, :, :], reads=[b_kc], writes=[bK])
                    for r in range(NCORES):
                        t.dma("sp", KT_[:, CTX + r * TOK:CTX + (r + 1) * TOK], kvga[r * 4096 + hm * 128:r * 4096 + (hm + 1) * 128, :],
                              reads=[b_kvg], partial=[bK])
                    t.dma("sp", QT_[:], qT[hm, :, :], reads=[b_qT], writes=[bQ])
                    qgroups = [(qg * 512, 512, list(range(NKC))) for qg in range(4)]
                    if not last:
                        qgroups.append((TOK, CTX, [0, 1]))
                    for (q0, qn, kchunks) in qgroups:
                        nqs = qn // 128
                        pend = []

                        def flush_deferred():
                            while deferred:
                                (d_od, d_tok, d_ch, d_bod, d_ot, d_bot) = deferred.pop(0)
                                emit_outT(d_od, 256, d_tok, 128, d_ch, d_bod, (6, b_ps[6], d_ot, d_bot))

                        def emit_pv(item, nk=len(kchunks), nqs=nqs):
                            (P_, bP, kc, ki) = item
                            for qs in range(nqs):
                                t.op("pe", lambda e, qs=qs: e.matmul(
                                    ps[2 + qs][:, 0:257], lhsT=P_[:, qs * 128:(qs + 1) * 128], rhs=V[:, kc, :],
                                    start=(ki == 0), stop=(ki == nk - 1)),
                                    reads=[bP, b_V], writes=([b_ps[2 + qs]] if ki == 0 else []),
                                    partial=([] if ki == 0 else [b_ps[2 + qs]]))

                        for ki, kc in enumerate(kchunks):
                            sb_ = SBANK[si % 3]
                            si += 1
                            P_, bP = PT[pti % 4], b_PT[pti % 4]
                            pti += 1
                            t.op("pe", lambda e, kc=kc, sb_=sb_, q0=q0, qn=qn, KT_=KT_, QT_=QT_: e.matmul(
                                ps[sb_][:, 0:qn], lhsT=KT_[:, kc * 128:(kc + 1) * 128], rhs=QT_[:, q0:q0 + qn],
                                start=True, stop=True), reads=[bK, bQ], writes=[b_ps[sb_]])
                            t.op("act", lambda e, P_=P_, sb_=sb_, qn=qn: e.activation(
                                out=P_[:, 0:qn], in_=ps[sb_][:, 0:qn], func=AF.Exp), reads=[b_ps[sb_]], writes=[bP])
                            pend.append((P_, bP, kc, ki))
                            if len(pend) > 2:
                                emit_pv(pend.pop(0))
                            if ki == 8:
                                flush_deferred()
                        while pend:
                            emit_pv(pend.pop(0))
                        flush_deferred()
                        for qs in range(nqs):
                            qt = q0 // 128 + qs
                            pso = ps[2 + qs]
                            bpo = b_ps[2 + qs]
                            sm_, bsm = sm[odi % 2], b_sm[odi % 2]
                            t.op("dve", lambda e, pso=pso, sm_=sm_: e.reciprocal(out=sm_[:, 0:1], in_=pso[:, 256:257]),
                                 reads=[bpo], writes=[bsm])
                            if c == 0:
                                t.op("dve", lambda e, pso=pso, sm_=sm_, qt=qt: e.tensor_scalar(
                                    out=O1[:, qt, :], in0=pso[:, 0:256], scalar1=sm_[:, 0:1], scalar2=None, op0=ALU.mult),
                                    reads=[bpo, bsm], partial=[b_O1])
                                odi += 1
                                continue
                            od_, bod = od[odi % 4], b_od[odi % 4]
                            ot_, bot = otb[odi % 4], b_otb[odi % 4]
                            odi += 1
                            t.op("dve", lambda e, sm_=sm_: e.tensor_tensor(out=sm_[:, 1:2], in0=sm_[:, 0:1], in1=lams[:, 5:6], op=ALU.mult),
                                 reads=[bsm, b_lam], writes=[bsm])
                            t.op("dve", lambda e, pso=pso, sm_=sm_, od_=od_, qt=qt: e.scalar_tensor_tensor(
                                out=od_[:], in0=pso[:, 0:256], scalar=sm_[:, 1:2], in1=O1[:, qt, :], op0=ALU.mult, op1=ALU.add),
                                reads=[bpo, bsm, b_O1], writes=[bod])
                            t.op("act", lambda e, od_=od_, sm_=sm_: e.activation(out=sq[:], in_=od_[:], func=AF.Square, accum_out=sm_[:, 2:3]),
                                 reads=[bod], writes=[b_sq, bsm])
                            t.op("act", lambda e, sm_=sm_: e.activation(out=sm_[:, 3:4], in_=sm_[:, 2:3], func=AF.Sqrt, bias=epsT[:, 0:1], scale=1.0 / 256),
                                 reads=[bsm, b_const], writes=[bsm])
                            t.op("dve", lambda e, sm_=sm_: e.reciprocal(out=sm_[:, 3:4], in_=sm_[:, 3:4]), reads=[bsm], writes=[bsm])
                            t.op("dve", lambda e, od_=od_, sm_=sm_: e.scalar_tensor_tensor(
                                out=od_[:], in0=od_[:], scalar=sm_[:, 3:4], in1=gsub[:], op0=ALU.mult, op1=ALU.mult),
                                reads=[bod, bsm, b_lam], writes=[bod])
                            deferred.append((od_, qt * 128, h * 2, bod, ot_, bot))
            while deferred:
                (d_od, d_tok, d_ch, d_bod, d_ot, d_bot) = deferred.pop(0)
                emit_outT(d_od, 256, d_tok, 128, d_ch, d_bod, (6, b_ps[6], d_ot, d_bot))
            t.barrier()

        with contextlib.ExitStack() as ls:
            KN = ls.enter_context(ust("KN", [128, 8, 24 * 128], BF16))
            VN = ls.enter_context(ust("VN", [128, 24, 8, 129], BF16))
            cand = [ls.enter_context(ust("cand%d" % i, [128, 8, 384], BF16)) for i in range(2)]
            candv = [ls.enter_context(ust("candv%d" % i, [128, 8, 1024], BF16)) for i in range(2)]
            QN = [ls.enter_context(ust("QN%d" % i, [128, NT], BF16)) for i in range(2)]
            bia = [ls.enter_context(ust("bia%d" % i, [128, 7, 128], F32)) for i in range(2)]
            big = [ls.enter_context(ust("big%d" % i, [128, 7, 128], F32)) for i in range(2)]
            msk = ls.enter_context(ust("msk", [128, 5, 7, 128], F32))
            ssb = [ls.enter_context(ust("ssb%d" % i, [128, 7, 128], F32)) for i in range(2)]
            PN = [ls.enter_context(ust("PN%d" % i, [128, 9, 128], BF16)) for i in range(2)]
            onb = [ls.enter_context(ust("onb%d" % i, [128, 128], F32)) for i in range(2)]
            smn = [ls.enter_context(ust("smn%d" % i, [128, 2], F32)) for i in range(2)]
            otn = [ls.enter_context(ust("otn%d" % i, [128, 2, 128], BF16)) for i in range(2)]
            b_KN, b_VN, b_msk = B(), B(), B()
            b_cand = [B(), B()]
            b_candv = [B(), B()]
            b_QN = [B(), B()]
            b_bia = [B(), B()]
            b_big = [B(), B()]
            b_ssb = [B(), B()]
            b_PN = [B(), B()]
            b_onb = [B(), B()]
            b_smn = [B(), B()]
            b_otn = [B(), B()]
            t.dma("sp", msk[:], nam_in.rearrange("v p c q -> p v c q"), writes=[b_msk])
            t.op("pool", lambda e: e.memset(VN[:, :, :, 128:129], 1.0), writes=[b_VN])
            for h in range(8):
                t.dma("sp", KN[:, h, 3 * 128:19 * 128], kva[1024 + h * 128:1024 + (h + 1) * 128, :], reads=[b_kv], partial=[b_KN])
                t.dma("sp", KN[:, h, 22 * 128:24 * 128], kcT[8 + h, :, :], reads=[b_kc], partial=[b_KN])
                t.dma("sp", VN[:, 3:19, h, 0:128], kva[2048:4096, 1024 + h * 128:1024 + (h + 1) * 128].rearrange("(c p) e -> p c e", p=128),
                      reads=[b_kv], partial=[b_VN])
                t.dma("sp", VN[:, 22:24, h, 0:128], vc[:, 1024 + h * 128:1024 + (h + 1) * 128].rearrange("(c p) e -> p c e", p=128),
                      reads=[b_kc], partial=[b_VN])
            ci = 0
            pbi = 0
            for side in range(2):
                c0 = TOK - 384 if side == 0 else 0
                dst0 = 0 if side == 0 else 19
                for h in range(8):
                    cd, bc = cand[ci % 2], b_cand[ci % 2]
                    ci += 1
                    t.dma("sp", cd[:], kvga.rearrange("(r x) t -> x r t", r=NCORES)[1024 + h * 128:1024 + (h + 1) * 128, :, c0:c0 + 384],
                          reads=[b_kvg], writes=[bc])
                    pb = pbi % 4
                    pbi += 1
                    for r in range(NCORES):
                        t.op("pe", lambda e, r=r, cd=cd, pb=pb, side=side: e.matmul(
                            ps[pb][:, 0:384], lhsT=selI[:, side * 8 + r, :], rhs=cd[:, r, :], start=(r == 0), stop=(r == NCORES - 1)),
                            reads=[bc, b_const], writes=([b_ps[pb]] if r == 0 else []), partial=([] if r == 0 else [b_ps[pb]]))
                    t.op("act", lambda e, h=h, pb=pb, dst0=dst0: e.copy(out=KN[:, h, dst0 * 128:(dst0 + 3) * 128], in_=ps[pb][:, 0:384]),
                         reads=[b_ps[pb]], partial=[b_KN])
                for cc in range(3):
                    cd, bc = candv[ci % 2], b_candv[ci % 2]
                    ci += 1
                    t.dma("sp", cd[:], kvga.rearrange("(r x) t -> x r t", r=NCORES)[2048 + c0 + cc * 128:2048 + c0 + (cc + 1) * 128, :, 1024:2048],
                          reads=[b_kvg], writes=[bc])
                    for half in range(2):
                        pb = pbi % 4
                        pbi += 1
                        for r in range(NCORES):
                            t.op("pe", lambda e, r=r, cd=cd, pb=pb, side=side, half=half: e.matmul(
                                ps[pb][:, :], lhsT=selI[:, side * 8 + r, :], rhs=cd[:, r, half * 512:(half + 1) * 512],
                                start=(r == 0), stop=(r == NCORES - 1)),
                                reads=[bc, b_const], writes=([b_ps[pb]] if r == 0 else []), partial=([] if r == 0 else [b_ps[pb]]))
                        t.op("act", lambda e, pb=pb, half=half, cc=cc, dst0=dst0: e.copy(
                            out=VN[:, dst0 + cc, half * 4:(half + 1) * 4, 0:128], in_=ps[pb][:, :].rearrange("p (h e) -> p h e", e=128)),
                            reads=[b_ps[pb]], partial=[b_VN])
            ni = 0
            for h in range(8):
                QN_, bQ = QN[h % 2], b_QN[h % 2]
                bi_, bbi = bia[h % 2], b_bia[h % 2]
                bg_, bbg = big[h % 2], b_big[h % 2]
                t.dma("sp", QN_[:], qT[8 + h, :, :], reads=[b_qT], writes=[bQ])
                t.dma("sp", bi_[:], nab_in[l, h, :, :, :], writes=[bbi])
                t.op("pool", lambda e, bi_=bi_, bg_=bg_: e.tensor_tensor(out=bg_[:], in0=bi_[:], in1=msk[:, 2, :, :], op=ALU.add),
                     reads=[bbi, b_msk], writes=[bbg])
                pend_b = None
                for qt in range(16):
                    var = {0: 0, 1: 1, 14: 3, 15: 4}.get(qt, 2)
                    s_, bs = ssb[ni % 2], b_ssb[ni % 2]
                    P_, bP = PN[ni % 2], b_PN[ni % 2]
                    on_, bon = onb[ni % 2], b_onb[ni % 2]
                    sn_, bsn = smn[ni % 2], b_smn[ni % 2]
                    ot_, bot = otn[ni % 2], b_otn[ni % 2]
                    pa = (ni % 2) * 4
                    ni += 1
                    for cch in range(7):
                        bk = pa + (cch // 4)
                        t.op("pe", lambda e, cch=cch, bk=bk, h=h, qt=qt, QN_=QN_: e.matmul(
                            ps[bk][:, (cch % 4) * 128:(cch % 4 + 1) * 128], lhsT=KN[:, h, (qt + cch) * 128:(qt + cch + 1) * 128],
                            rhs=QN_[:, qt * 128:(qt + 1) * 128], start=True, stop=True),
                            reads=[b_KN, bQ], writes=([b_ps[bk]] if cch % 4 == 0 else []), partial=([] if cch % 4 == 0 else [b_ps[bk]]))
                    for cch in range(2):
                        t.op("pe", lambda e, cch=cch, pa=pa, h=h, qt=qt, QN_=QN_: e.matmul(
                            ps[pa + 2][:, cch * 128:(cch + 1) * 128], lhsT=KN[:, h, (22 + cch) * 128:(23 + cch) * 128],
                            rhs=QN_[:, qt * 128:(qt + 1) * 128], start=True, stop=True),
                            reads=[b_KN, bQ], writes=([b_ps[pa + 2]] if cch == 0 else []), partial=([] if cch == 0 else [b_ps[pa + 2]]))
                    bsrc, bbsrc = (bg_, bbg) if var == 2 else (bi_, bbi)
                    t.op("dve", lambda e, s_=s_, pa=pa, bsrc=bsrc: e.tensor_tensor(
                        out=s_[:, 0:4, :], in0=ps[pa][:, :].rearrange("p (c q) -> p c q", q=128), in1=bsrc[:, 0:4, :], op=ALU.add),
                        reads=[b_ps[pa], bbsrc], writes=[bs])
                    t.op("dve", lambda e, s_=s_, pa=pa, bsrc=bsrc: e.tensor_tensor(
                        out=s_[:, 4:7, :], in0=ps[pa + 1][:, 0:384].rearrange("p (c q) -> p c q", q=128), in1=bsrc[:, 4:7, :], op=ALU.add),
                        reads=[b_ps[pa + 1], bbsrc], partial=[bs])
                    if var != 2:
                        t.op("pool", lambda e, s_=s_, var=var: e.tensor_tensor(out=s_[:], in0=s_[:], in1=msk[:, var, :, :], op=ALU.add),
                             reads=[bs, b_msk], writes=[bs])
                    t.op("act", lambda e, s_=s_, P_=P_: e.activation(out=P_[:, 0:7, :], in_=s_[:], func=AF.Exp), reads=[bs], writes=[bP])
                    t.op("act", lambda e, pa=pa, P_=P_: e.activation(out=P_[:, 7:9, :], in_=ps[pa + 2][:, 0:256].rearrange("p (c q) -> p c q", q=128),
                                                                  func=AF.Exp), reads=[b_ps[pa + 2]], partial=[bP])
                    def phase_b(qt=qt, pa=pa, P_=P_, bP=bP, on_=on_, bon=bon, sn_=sn_, bsn=bsn, ot_=ot_, bot=bot, h=h):
                        for cch in range(9):
                            vch = (qt + cch) if cch < 7 else (22 + cch - 7)
                            t.op("pe", lambda e, cch=cch, vch=vch: e.matmul(
                                ps[pa + 3][:, 0:129], lhsT=P_[:, cch, :], rhs=VN[:, vch, h, :], start=(cch == 0), stop=(cch == 8)),
                                reads=[bP, b_VN], writes=([b_ps[pa + 3]] if cch == 0 else []), partial=([] if cch == 0 else [b_ps[pa + 3]]))
                        t.op("dve", lambda e: e.reciprocal(out=sn_[:, 0:1], in_=ps[pa + 3][:, 128:129]), reads=[b_ps[pa + 3]], writes=[bsn])
                        t.op("dve", lambda e: e.tensor_scalar(
                            out=on_[:], in0=ps[pa + 3][:, 0:128], scalar1=sn_[:, 0:1], scalar2=None, op0=ALU.mult),
                            reads=[b_ps[pa + 3], bsn], writes=[bon])
                        emit_outT(on_, 128, qt * 128, 128, 8 + h, bon, (pa + 2, b_ps[pa + 2], ot_, bot))

                    if pend_b is not None:
                        pend_b()
                    pend_b = phase_b
                if pend_b is not None:
                    pend_b()
                    pend_b = None
                if not last:
                    for qs in range(2):
                        s_, bs = ssb[ni % 2], b_ssb[ni % 2]
                        P_, bP = PN[ni % 2], b_PN[ni % 2]
                        on_, bon = onb[ni % 2], b_onb[ni % 2]
                        sn_, bsn = smn[ni % 2], b_smn[ni % 2]
                        ot_, bot = otn[ni % 2], b_otn[ni % 2]
                        pa = (ni % 2) * 4
                        ni += 1
                        q0 = TOK + qs * 128
                        for cch in range(2):
                            t.op("pe", lambda e, cch=cch, pa=pa, h=h, q0=q0, QN_=QN_: e.matmul(
                                ps[pa + 2][:, cch * 128:(cch + 1) * 128], lhsT=KN[:, h, (22 + cch) * 128:(23 + cch) * 128],
                                rhs=QN_[:, q0:q0 + 128], start=True, stop=True),
                                reads=[b_KN, bQ], writes=([b_ps[pa + 2]] if cch == 0 else []), partial=([] if cch == 0 else [b_ps[pa + 2]]))
                        t.op("act", lambda e, pa=pa, P_=P_: e.activation(out=P_[:, 7:9, :], in_=ps[pa + 2][:, 0:256].rearrange("p (c q) -> p c q", q=128),
                                                                      func=AF.Exp), reads=[b_ps[pa + 2]], writes=[bP])
                        for cch in range(2):
                            t.op("pe", lambda e, cch=cch, pa=pa, P_=P_, h=h: e.matmul(
                                ps[pa + 3][:, 0:129], lhsT=P_[:, 7 + cch, :], rhs=VN[:, 22 + cch, h, :], start=(cch == 0), stop=(cch == 1)),
                                reads=[bP, b_VN], writes=([b_ps[pa + 3]] if cch == 0 else []), partial=([] if cch == 0 else [b_ps[pa + 3]]))
                        t.op("dve", lambda e, pa=pa, sn_=sn_: e.reciprocal(out=sn_[:, 0:1], in_=ps[pa + 3][:, 128:129]), reads=[b_ps[pa + 3]], writes=[bsn])
                        t.op("dve", lambda e, pa=pa, sn_=sn_, on_=on_: e.tensor_scalar(
                            out=on_[:], in0=ps[pa + 3][:, 0:128], scalar1=sn_[:, 0:1], scalar2=None, op0=ALU.mult),
                            reads=[b_ps[pa + 3], bsn], writes=[bon])
                        emit_outT(on_, 128, q0, 128, 8 + h, bon, (pa + 2, b_ps[pa + 2], ot_, bot))
            t.barrier()

    def layer_norm(rr, b_rr, n, sqb, b_sqb, stat, b_stat, pbm, pbq):
        for j in range(16):
            t.op("pe", lambda e, j=j: e.matmul(ps[pbm][:, 0:n], lhsT=onesm[:], rhs=rr[:, j, 0:n], start=(j == 0), stop=(j == 15)),
                 reads=[b_rr, b_const], writes=([b_ps[pbm]] if j == 0 else []), partial=([] if j == 0 else [b_ps[pbm]]))
        for j in range(16):
            s_ = sqb[j % 2]
            t.op("act", lambda e, j=j, s_=s_: e.activation(out=s_[:, 0:n], in_=rr[:, j, 0:n], func=AF.Square),
                 reads=[b_rr], writes=[b_sqb[j % 2]])
            t.op("pe", lambda e, j=j, s_=s_: e.matmul(ps[pbq][:, 0:n], lhsT=onesm[:], rhs=s_[:, 0:n], start=(j == 0), stop=(j == 15)),
                 reads=[b_sqb[j % 2], b_const], writes=([b_ps[pbq]] if j == 0 else []), partial=([] if j == 0 else [b_ps[pbq]]))
        t.op("act", lambda e: e.copy(out=stat[:, 0, 0:n], in_=ps[pbm][:, 0:n]), reads=[b_ps[pbm]], writes=[b_stat])
        t.op("dve", lambda e: e.tensor_tensor(out=stat[:, 1, 0:n], in0=stat[:, 0, 0:n], in1=stat[:, 0, 0:n], op=ALU.mult),
             reads=[b_stat], writes=[b_stat])
        t.op("dve", lambda e: e.tensor_tensor(out=stat[:, 1, 0:n], in0=ps[pbq][:, 0:n], in1=stat[:, 1, 0:n], op=ALU.subtract),
             reads=[b_stat, b_ps[pbq]], writes=[b_stat])
        t.op("act", lambda e: e.activation(out=stat[:, 1, 0:n], in_=stat[:, 1, 0:n], func=AF.Sqrt, bias=epsT[:, 0:1], scale=1.0),
             reads=[b_stat, b_const], writes=[b_stat])
        t.op("dve", lambda e: e.reciprocal(out=stat[:, 1, 0:n], in_=stat[:, 1, 0:n]), reads=[b_stat], writes=[b_stat])
        for j in range(16):
            eng = "dve" if j % 2 else "pool"
            t.op(eng, lambda e, j=j: e.tensor_tensor(out=rr[:, j, 0:n], in0=rr[:, j, 0:n], in1=stat[:, 0, 0:n], op=ALU.subtract),
                 reads=[b_stat, b_rr], partial=[b_rr])
            t.op(eng, lambda e, j=j: e.tensor_tensor(out=rr[:, j, 0:n], in0=rr[:, j, 0:n], in1=stat[:, 1, 0:n], op=ALU.mult),
                 reads=[b_stat, b_rr], partial=[b_rr])

    def stage_oproj(l, last):
        with contextlib.ExitStack() as ls:
            og = ls.enter_context(ust("og", [128, 16, 410], BF16))
            xg = ls.enter_context(ust("xg3", [128, 16, 410], F32))
            rr = ls.enter_context(ust("rr3", [128, 16, 410], F32))
            h2 = ls.enter_context(ust("h2_3", [128, 16, 410], BF16))
            ws = [ls.enter_context(ust("wo%d" % i, [128, 16, 512], BF16)) for i in range(2)]
            sqb = [ls.enter_context(ust("sqb3_%d" % i, [128, 410], F32)) for i in range(2)]
            stat = ls.enter_context(ust("stat3", [128, 2, 410], F32))
            tmp = [ls.enter_context(ust("tmp3_%d" % i, [128, 410], F32)) for i in range(2)]
            hbs = ls.enter_context(ust("hbs", [128, 2, 16], BF16))
            hcand = ls.enter_context(ust("hcand", [128, NCORES, 32], BF16))
            hacc = ls.enter_context(ust("hacc", [128, 2, 16], F32))
            b_og, b_xg, b_rr, b_h2, b_stat, b_hbs, b_hc, b_hacc = B(), B(), B(), B(), B(), B(), B(), B()
            b_ws = [B(), B()]
            b_sqb = [B(), B()]
            b_tmp = [B(), B()]
            wi = 0
            ti_ = 0
            first_h2 = True
            for g, (s0, n) in enumerate(GROUPS):
                isctx = (g == len(GROUPS) - 1)
                if isctx and last:
                    continue
                tt = 1 if isctx else 0
                t.dma("sp", og[:, :, 0:n], oT[:, :, s0:s0 + n].rearrange("j p t -> p j t"), reads=[b_oT], writes=[b_og])
                t.dma("sp", xg[:, :, 0:n], xT[:, :, s0:s0 + n].rearrange("j p t -> p j t"), reads=[b_xT[g]], writes=[b_xg])
                for nb in range(4):
                    w, bw = ws[wi % 2], b_ws[wi % 2]
                    wi += 1
                    t.dma("sp", w[:], w_o[l][:, nb * 512:(nb + 1) * 512].rearrange("(k p) n -> p k n", p=128), reads=[b_wg["w_o"][l]], writes=[bw])
                    for s in range(4):
                        j = nb * 4 + s
                        pb = j % 4
                        for k in range(16):
                            t.op("pe", lambda e, k=k, s=s, pb=pb, w=w: e.matmul(
                                ps[pb][:, 0:n], lhsT=w[:, k, s * 128:(s + 1) * 128], rhs=og[:, k, 0:n], start=(k == 0), stop=(k == 15)),
                                reads=[bw, b_og], writes=([b_ps[pb]] if k == 0 else []), partial=([] if k == 0 else [b_ps[pb]]))
                        tm, btm = tmp[ti_ % 2], b_tmp[ti_ % 2]
                        ti_ += 1
                        t.op("act", lambda e, tm=tm, pb=pb, j=j, tt=tt: e.activation(
                            out=tm[:, 0:n], in_=ps[pb][:, 0:n], func=AF.Copy, scale=prm[:, l, 2, j, tt:tt + 1]),
                            reads=[b_ps[pb], b_prm], writes=[btm])
                        t.op("dve", lambda e, tm=tm, j=j: e.scalar_tensor_tensor(
                            out=rr[:, j, 0:n], in0=xg[:, j, 0:n], scalar=ALPHA, in1=tm[:, 0:n], op0=ALU.mult, op1=ALU.add),
                            reads=[btm, b_xg], writes=([b_rr] if j == 0 else []), partial=([] if j == 0 else [b_rr]))
                layer_norm(rr, b_rr, n, sqb, b_sqb, stat, b_stat, 4, 5)
                for j in range(16):
                    eng = "pool" if j % 2 else "dve"
                    t.op(eng, lambda e, j=j, tt=tt: e.tensor_scalar(
                        out=h2[:, j, 0:n], in0=rr[:, j, 0:n], scalar1=prm[:, l, 3, j, tt:tt + 1], scalar2=prm[:, l, 4, j, tt:tt + 1],
                        op0=ALU.mult, op1=ALU.add), reads=[b_rr, b_prm], writes=([b_h2] if j == 0 else []), partial=([] if j == 0 else [b_h2]))
                    t.op(eng, lambda e, j=j: e.tensor_scalar(
                        out=xg[:, j, 0:n], in0=rr[:, j, 0:n], scalar1=lnp[:, l, 0, j:j + 1], scalar2=lnp[:, l, 1, j:j + 1],
                        op0=ALU.mult, op1=ALU.add), reads=[b_rr, b_const], writes=([b_xg] if j == 0 else []), partial=([] if j == 0 else [b_xg]))
                t.dma("sp", xT[:, :, s0:s0 + n].rearrange("j p t -> p j t"), xg[:, :, 0:n], reads=[b_xg], writes=[b_xT[g]])
                t.dma("sp", h2T[:, :, s0:s0 + n].rearrange("j p t -> p j t"), h2[:, :, 0:n], reads=[b_h2], partial=[b_h2T])
                if g == 0:
                    t.op("dve", lambda e: e.tensor_copy(out=hbs[:, 0, :], in_=h2[:, :, 0]), reads=[b_h2], writes=[b_hbs])
                if g == len(GROUPS) - 2:
                    t.op("dve", lambda e, n=n: e.tensor_copy(out=hbs[:, 1, :], in_=h2[:, :, n - 1]), reads=[b_h2], partial=[b_hbs])
            t.dma("sp", hb.ap()[:, :], hbs[:].rearrange("p a j -> p (a j)"), reads=[b_hbs], writes=[b_hb])
            t.allgather(hb.ap().opt(), hbg.ap().opt(), reads=[b_hb], writes=[b_hbg])
            t.dma("sp", hcand[:], hbg.ap().rearrange("(r p) c -> p r c", p=128), reads=[b_hbg], writes=[b_hc])
            for side in range(2):
                c0 = 16 if side == 0 else 0
                for r in range(NCORES):
                    if r == 0:
                        t.op("dve", lambda e, side=side, c0=c0: e.tensor_scalar(
                            out=hacc[:, side, :], in0=hcand[:, 0, c0:c0 + 16], scalar1=selb[:, side * 8:side * 8 + 1], scalar2=None, op0=ALU.mult),
                            reads=[b_hc, b_const], writes=([b_hacc] if side == 0 else []), partial=([] if side == 0 else [b_hacc]))
                    else:
                        t.op("dve", lambda e, side=side, c0=c0, r=r: e.scalar_tensor_tensor(
                            out=hacc[:, side, :], in0=hcand[:, r, c0:c0 + 16], scalar=selb[:, side * 8 + r:side * 8 + r + 1], in1=hacc[:, side, :],
                            op0=ALU.mult, op1=ALU.add), reads=[b_hc, b_const, b_hacc], writes=[b_hacc])
            t.op("dve", lambda e: e.tensor_copy(out=hh[:], in_=hacc[:]), reads=[b_hacc], writes=[b_hh])
            t.barrier()

    def stage_ffn(l, last):
        with contextlib.ExitStack() as ls:
            hw = ls.enter_context(ust("hw", [128, 16, 412], BF16))
            aT = ls.enter_context(ust("aT", [128, NFC, 410], BF16))
            xg = ls.enter_context(ust("xg4", [128, 16, 410], F32))
            rr = ls.enter_context(ust("rr4", [128, 16, 410], F32))
            wu = [ls.enter_context(ust("wu%d" % i, [128, 16, 512], BF16)) for i in range(4)]
            sqb = [ls.enter_context(ust("sqb4_%d" % i, [128, 410], F32)) for i in range(2)]
            stat = ls.enter_context(ust("stat4", [128, 2, 410], F32))
            tmp = [ls.enter_context(ust("tmp4_%d" % i, [128, 410], F32)) for i in range(2)]
            cv = [ls.enter_context(ust("cv%d" % i, [128, 410], F32)) for i in range(2)]
            sg = [ls.enter_context(ust("sg%d" % i, [128, 410], F32)) for i in range(2)]
            xo = [ls.enter_context(ust("xo4_%d" % i, [128, D], F32)) for i in range(2)] if last else None
            b_hw, b_aT, b_xg, b_rr, b_stat = B(), B(), B(), B(), B()
            b_wu = [B() for _ in range(4)]
            b_sqb = [B(), B()]
            b_tmp = [B(), B()]
            b_cv = [B(), B()]
            b_sg = [B(), B()]
            b_xo = [B(), B()]
            wi = 0
            ci = 0
            ti_ = 0
            xi = 0
            for g, (s0, n) in enumerate(GROUPS):
                isctx = (g == len(GROUPS) - 1)
                if isctx and last:
                    continue
                tt = 1 if isctx else 0
                lo = 1 if (g == 0 or isctx) else 0
                hi = 1 if (g >= len(GROUPS) - 2) else 0
                t.dma("sp", hw[:, :, lo:n + 2 - hi], h2T[:, :, s0 - 1 + lo:s0 + n + 1 - hi].rearrange("j p t -> p j t"),
                      reads=[b_h2T], writes=[b_hw])
                if isctx:
                    t.op("dve", lambda e: e.memset(hw[:, :, 0:1], 0.0), partial=[b_hw])
                    t.op("dve", lambda e, n=n: e.memset(hw[:, :, n + 1:n + 2], 0.0), partial=[b_hw])
                else:
                    if lo:
                        t.op("dve", lambda e: e.tensor_copy(out=hw[:, :, 0], in_=hh[:, 0, :]), reads=[b_hh], partial=[b_hw])
                    if hi:
                        t.op("dve", lambda e, n=n: e.tensor_copy(out=hw[:, :, n + 1], in_=hh[:, 1, :]), reads=[b_hh], partial=[b_hw])
                t.dma("sp", xg[:, :, 0:n], xT[:, :, s0:s0 + n].rearrange("j p t -> p j t"), reads=[b_xT[g]], writes=[b_xg])
                for fb in range(11):
                    wg, bwg = wu[wi % 4], b_wu[wi % 4]
                    wuu, bwu = wu[(wi + 1) % 4], b_wu[(wi + 1) % 4]
                    wi += 2
                    t.dma("sp", wg[:], w_up[l][:, fb * 512:(fb + 1) * 512].rearrange("(k p) n -> p k n", p=128), reads=[b_wg["w_up"][l]], writes=[bwg])
                    t.dma("sp", wuu[:], w_up[l][:, DFF + fb * 512:DFF + (fb + 1) * 512].rearrange("(k p) n -> p k n", p=128), reads=[b_wg["w_up"][l]], writes=[bwu])
                    for s in range(4):
                        fc = fb * 4 + s
                        pg = (fc % 2) * 2
                        pu = pg + 1
                        for k in range(16):
                            t.op("pe", lambda e, k=k, s=s, pg=pg, wg=wg: e.matmul(
                                ps[pg][:, 0:n + 2], lhsT=wg[:, k, s * 128:(s + 1) * 128], rhs=hw[:, k, 0:n + 2], start=(k == 0), stop=(k == 15)),
                                reads=[bwg, b_hw], writes=([b_ps[pg]] if k == 0 else []), partial=([] if k == 0 else [b_ps[pg]]))
                        for k in range(16):
                            t.op("pe", lambda e, k=k, s=s, pu=pu, wuu=wuu: e.matmul(
                                ps[pu][:, 0:n], lhsT=wuu[:, k, s * 128:(s + 1) * 128], rhs=hw[:, k, 1:n + 1], start=(k == 0), stop=(k == 15)),
                                reads=[bwu, b_hw], writes=([b_ps[pu]] if k == 0 else []), partial=([] if k == 0 else [b_ps[pu]]))
                        c_, bc = cv[ci % 2], b_cv[ci % 2]
                        s_, bs = sg[ci % 2], b_sg[ci % 2]
                        ci += 1
                        t.op("dve", lambda e, c_=c_, pg=pg, fc=fc: e.tensor_scalar(
                            out=c_[:, 0:n], in0=ps[pg][:, 0:n], scalar1=convp[:, l, 0, fc:fc + 1], scalar2=None, op0=ALU.mult),
                            reads=[b_ps[pg], b_const], writes=[bc])
                        t.op("dve", lambda e, c_=c_, pg=pg, fc=fc: e.scalar_tensor_tensor(
                            out=c_[:, 0:n], in0=ps[pg][:, 1:n + 1], scalar=convp[:, l, 1, fc:fc + 1], in1=c_[:, 0:n], op0=ALU.mult, op1=ALU.add),
                            reads=[b_ps[pg], b_const, bc], writes=[bc])
                        t.op("dve", lambda e, c_=c_, pg=pg, fc=fc: e.scalar_tensor_tensor(
                            out=c_[:, 0:n], in0=ps[pg][:, 2:n + 2], scalar=convp[:, l, 2, fc:fc + 1], in1=c_[:, 0:n], op0=ALU.mult, op1=ALU.add),
                            reads=[b_ps[pg], b_const, bc], writes=[bc])
                        t.op("act", lambda e, c_=c_, s_=s_, fc=fc: e.activation(
                            out=s_[:, 0:n], in_=c_[:, 0:n], func=AF.Silu, bias=convp[:, l, 3, fc:fc + 1]),
                            reads=[bc, b_const], writes=[bs])
                        t.op("dve", lambda e, s_=s_, pu=pu, fc=fc: e.tensor_tensor(
                            out=aT[:, fc, 0:n], in0=ps[pu][:, 0:n], in1=s_[:, 0:n], op=ALU.mult),
                            reads=[b_ps[pu], bs], writes=([b_aT] if fc == 0 else []), partial=([] if fc == 0 else [b_aT]))
                for nb in range(8):
                    wd, bwd = wu[wi % 4], b_wu[wi % 4]
                    wd2, bwd2 = wu[(wi + 1) % 4], b_wu[(wi + 1) % 4]
                    wi += 2
                    wv = [wd[:, :, :].rearrange("p a b -> p (a b)")[:, 0:22 * 256].rearrange("p (f n) -> p f n", n=256),
                          wd2[:, :, :].rearrange("p a b -> p (a b)")[:, 0:22 * 256].rearrange("p (f n) -> p f n", n=256)]
                    for hf in range(2):
                        t.dma("sp", wv[hf], w_down[l][hf * 22 * 128:(hf + 1) * 22 * 128, nb * 256:(nb + 1) * 256].rearrange("(f p) n -> p f n", p=128),
                              reads=[b_wg["w_down"][l]], writes=[[bwd, bwd2][hf]])
                    for s in range(2):
                        j = nb * 2 + s
                        pb = 4 + (j % 2)
                        for fc in range(NFC):
                            t.op("pe", lambda e, fc=fc, s=s, pb=pb, wv=wv: e.matmul(
                                ps[pb][:, 0:n], lhsT=wv[fc // 22][:, fc % 22, s * 128:(s + 1) * 128], rhs=aT[:, fc, 0:n],
                                start=(fc == 0), stop=(fc == NFC - 1)),
                                reads=[bwd, bwd2, b_aT], writes=([b_ps[pb]] if fc == 0 else []), partial=([] if fc == 0 else [b_ps[pb]]))
                        tm, btm = tmp[ti_ % 2], b_tmp[ti_ % 2]
                        ti_ += 1
                        t.op("act", lambda e, tm=tm, pb=pb, j=j, tt=tt: e.activation(
                            out=tm[:, 0:n], in_=ps[pb][:, 0:n], func=AF.Copy, scale=prm[:, l, 5, j, tt:tt + 1]),
                            reads=[b_ps[pb], b_prm], writes=[btm])
                        t.op("dve", lambda e, tm=tm, j=j: e.scalar_tensor_tensor(
                            out=rr[:, j, 0:n], in0=xg[:, j, 0:n], scalar=ALPHA, in1=tm[:, 0:n], op0=ALU.mult, op1=ALU.add),
                            reads=[btm, b_xg], writes=([b_rr] if j == 0 else []), partial=([] if j == 0 else [b_rr]))
                layer_norm(rr, b_rr, n, sqb, b_sqb, stat, b_stat, 6, 7)
                for j in range(16):
                    eng = "pool" if j % 2 else "dve"
                    t.op(eng, lambda e, j=j: e.tensor_scalar(
                        out=xg[:, j, 0:n], in0=rr[:, j, 0:n], scalar1=lnp[:, l, 2, j:j + 1], scalar2=lnp[:, l, 3, j:j + 1],
                        op0=ALU.mult, op1=ALU.add), reads=[b_rr, b_const], writes=([b_xg] if j == 0 else []), partial=([] if j == 0 else [b_xg]))
                if not last:
                    t.dma("sp", xT[:, :, s0:s0 + n].rearrange("j p t -> p j t"), xg[:, :, 0:n], reads=[b_xg], writes=[b_xT[g]])
                else:
                    for (ts, tn) in _subtiles(n):
                        xo_, bxo = xo[xi % 2], b_xo[xi % 2]
                        xi += 1
                        for jb in range(4):
                            pb = jb
                            for s in range(4):
                                j = jb * 4 + s
                                t.op("pe", lambda e, j=j, s=s, pb=pb, ts=ts, tn=tn: e.transpose(
                                    ps[pb][0:tn, s * 128:(s + 1) * 128], xg[:, j, ts:ts + tn], ident[:]),
                                    reads=[b_xg, b_const], writes=([b_ps[pb]] if s == 0 else []), partial=([] if s == 0 else [b_ps[pb]]))
                            t.op("act" if jb % 2 else "dve",
                                 (lambda e, jb=jb, pb=pb, xo_=xo_, tn=tn: e.copy(out=xo_[0:tn, jb * 512:(jb + 1) * 512], in_=ps[pb][0:tn, :])) if jb % 2 else
                                 (lambda e, jb=jb, pb=pb, xo_=xo_, tn=tn: e.tensor_copy(out=xo_[0:tn, jb * 512:(jb + 1) * 512], in_=ps[pb][0:tn, :])),
                                 reads=[b_ps[pb]], writes=([bxo] if jb == 0 else []), partial=([] if jb == 0 else [bxo]))
                        t.dma("sp", out[s0 + ts:s0 + ts + tn, :], xo_[0:tn, :], reads=[bxo], partial=[b_out])
            t.barrier()

    stop = dbg[0] if dbg is not None else None
    load_consts()
    stage_w()
    stage_mod()
    stage_in()
    done = False
    for l in range(n_layers):
        last = (l == DEPTH - 1)
        for name, fn in (("qkv", stage_qkv), ("attn", stage_attn), ("oproj", stage_oproj), ("ffn", stage_ffn)):
            fn(l, last)
            if stop == (l, name):
                done = True
                break
        if done:
            break
    if dbg is not None:
        tens = dict(xT=xT, qT=qT, kv=kva, kcT=kcT, vc=vc, oT=oT, h2T=h2T, hbg=hbg.ap(), modT=modT[:], prm=prm[:])
        for name in dbg[1]:
            src = tens[name]
            do = nc.dram_tensor("dbg_" + name, list(src.shape), src.dtype, kind="ExternalOutput").ap()
            n0 = src.shape[0]
            step = 1 if n0 <= 64 else 128
            if name in ("modT", "prm"):
                t.dma("sp", do, src, writes=[B()])
            else:
                for a in range(0, n0, step):
                    t.dma("pool", do[a:a + step], src[a:a + step], writes=[B()])
    t.finish("sp")
    es.close()
    return nc


def _rope_tables():
    tpos = np.arange(SEQ)
    pos_r = (tpos // GRID_W).astype(np.float32)
    pos_c = (tpos % GRID_W).astype(np.float32)
    half = HD // 2
    inv_freq = (1.0 / (10000.0 ** (np.arange(0, half, 2, dtype=np.float32) / half))).astype(np.float32)
    ar = pos_r[:, None] * inv_freq[None, :]
    ac = pos_c[:, None] * inv_freq[None, :]
    ang = np.concatenate([ar, ar, ac, ac], axis=-1).astype(np.float32)
    cos = np.cos(ang).astype(np.float32)
    sin = np.sin(ang).astype(np.float32)
    sign = np.concatenate([-np.ones(32), np.ones(32), -np.ones(32), np.ones(32)]).astype(np.float32)
    return cos, sin * sign[None, :]


def _perm_matrix():
    src = np.concatenate([np.arange(32, 64), np.arange(0, 32), np.arange(96, 128), np.arange(64, 96)])
    pm = np.zeros((128, 128), np.float32)
    pm[src, np.arange(128)] = 1.0
    return pm


def _na_tables(rpb):
    kk = np.arange(7 * 128)
    krow = kk // 64 - 6
    kcol = kk % 64
    qq = np.arange(128)
    qrow = qq // 64
    qcol = qq % 64
    dr = krow[:, None] - qrow[None, :] + 7
    dc = kcol[:, None] - qcol[None, :] + 15
    drc = np.clip(dr, 0, 14)
    dcc = np.clip(dc, 0, 30)
    bias = rpb[:, :, drc, dcc]
    bias = bias.reshape(DEPTH, 8, 7, 128, 128).transpose(0, 1, 3, 2, 4)
    cs = np.clip(qcol - 8, 0, GRID_W - 16)
    colok = (kcol[:, None] >= cs[None, :]) & (kcol[:, None] < cs[None, :] + 16)
    rows = SEQ // GRID_W
    masks = np.zeros((NCORES, 5, 7 * 128, 128), np.float32)
    for core in range(NCORES):
        for v, qt in enumerate([0, 1, 2, 14, 15]):
            r0 = core * 32 + 2 * qt
            r = r0 + qrow
            rs = np.clip(r - 4, 0, rows - 8)
            kr = r0 + krow
            rowok = (kr[:, None] >= rs[None, :]) & (kr[:, None] < rs[None, :] + 8)
            masks[core, v] = np.where(rowok & colok, 0.0, NEG)
    masks = masks.reshape(NCORES, 5, 7, 128, 128).transpose(0, 1, 3, 2, 4)
    return np.ascontiguousarray(bias.astype(np.float32)), np.ascontiguousarray(masks)


def _chunkT(v, nch):
    return np.ascontiguousarray(np.swapaxes(v.reshape(v.shape[:-1] + (nch, 128)), -1, -2))


def make_in_maps(x, c, ctx, c_ctx, w_ada, b_ada, w_in, da_lambda, da_subln, na_rpb, w_o,
                 ln1_g, ln1_b, w_up, conv_w, conv_b, w_down, ln2_g, ln2_b):
    f = lambda a: np.ascontiguousarray(np.asarray(a, dtype=np.float32))
    x, c, ctx, c_ctx = f(x), f(c), f(ctx), f(c_ctx)
    cos, sins = _rope_tables()
    wada = f(w_ada)
    big = dict(w_in=f(w_in), w_o=f(w_o), w_up=f(w_up), w_down=f(w_down))
    nab, nam = _na_tables(f(na_rpb))
    cT = np.stack([_chunkT(c[0], 16), _chunkT(c_ctx, 16)], axis=-1)
    lnp = np.stack([_chunkT(f(ln1_g), 16), _chunkT(f(ln1_b), 16), _chunkT(f(ln2_g), 16), _chunkT(f(ln2_b), 16)], axis=2)
    cw = f(conv_w)
    convp = np.stack([_chunkT(cw[:, 0], NFC), _chunkT(cw[:, 1], NFC), _chunkT(cw[:, 2], NFC), _chunkT(f(conv_b), NFC)], axis=2)
    shared = dict(
        ctx=ctx[0], cT=np.ascontiguousarray(cT), b_adaT=_chunkT(f(b_ada), 96),
        lam_bc=np.ascontiguousarray(np.broadcast_to(f(da_lambda).reshape(DEPTH, 1, 512), (DEPTH, 128, 512))),
        subln_bc=np.ascontiguousarray(np.broadcast_to(f(da_subln).reshape(DEPTH, 1, 256), (DEPTH, 128, 256))),
        lnp=np.ascontiguousarray(lnp), convp=np.ascontiguousarray(convp),
        perm=_perm_matrix(), ident=np.eye(128, dtype=np.float32), na_bias=nab)
    maps = []
    for i in range(NCORES):
        sl = slice(i * TOK, (i + 1) * TOK)
        sel = np.zeros((16,), np.float32)
        selh = np.zeros((16, 2), np.float32)
        if i > 0:
            sel[i - 1] = 1.0
            selh[(i - 1) * 2 + 1, 0] = 1.0
        if i < NCORES - 1:
            sel[8 + i + 1] = 1.0
            selh[(i + 1) * 2, 1] = 1.0
        m = dict(shared)
        for kk, vv in big.items():
            rp = vv.shape[1] // NCORES
            m[kk] = np.ascontiguousarray(vv[:, i * rp:(i + 1) * rp, :])
        m.update(x=np.ascontiguousarray(x[0, sl]),
                 w_ada=np.ascontiguousarray(wada[:, :, i * (6 * D // NCORES):(i + 1) * (6 * D // NCORES)]),
                 ropeT=np.ascontiguousarray(np.stack([cos[sl].T, sins[sl].T])),
                 na_mask=np.ascontiguousarray(nam[i]),
                 sel=np.ascontiguousarray(np.broadcast_to(sel[None, :], (128, 16))),
                 selh=selh)
        maps.append(m)
    return maps


def kernel(**inputs):
    maps = make_in_maps(**inputs)
    nc = build_nc()
    res = run_bass_kernel_spmd(nc, maps, core_ids=list(range(NCORES)))
    outs = [np.asarray(res.results[i]["out"], dtype=np.float32) for i in range(NCORES)]
    return np.concatenate(outs, axis=0)[None]
```
